# Optimizing a Trainium2 kernel written in Bass

```python
import math
import jax, jax.numpy as jnp
from jax import lax
import numpy as np

D_MODEL = 2048
BATCH = 2
SEQ = 16384
DEPTH = 2

D_MIX = D_MODEL
DIFF_WIDTH = D_MIX // 2
DIFF_HEAD_DIM = 128
DIFF_V_DIM = 2 * DIFF_HEAD_DIM
DIFF_HEADS = DIFF_WIDTH // DIFF_V_DIM
MLA_V = 128
MLA_HEADS = (D_MIX - DIFF_WIDTH) // MLA_V
MLA_NOPE = 128
MLA_ROPE = 64
Q_LORA = D_MODEL // 4
KV_LORA = D_MODEL // 8
D_FF = 4 * D_MODEL
NUM_BUCKETS = 32
MAX_DISTANCE = 128
ROPE_THETA = 10000.0
Q_BLOCK = 128
EPS = 1e-6

DIFF_Q_COLS = DIFF_HEADS * 2 * DIFF_HEAD_DIM
DIFF_K_COLS = DIFF_HEADS * 2 * DIFF_HEAD_DIM
DIFF_V_COLS = DIFF_HEADS * DIFF_V_DIM
OFF_DQ = 0
OFF_DK = OFF_DQ + DIFF_Q_COLS
OFF_DV = OFF_DK + DIFF_K_COLS
OFF_CQ = OFF_DV + DIFF_V_COLS
OFF_CKV = OFF_CQ + Q_LORA
OFF_KR = OFF_CKV + KV_LORA
IN_COLS = OFF_KR + MLA_ROPE

kernel_name = "hybrid_diffattn_mla_encoder"


def _rmsnorm(x, g=None):
    x32 = x.astype(jnp.float32)
    y = x32 * lax.rsqrt(jnp.mean(x32 * x32, axis=-1, keepdims=True) + EPS)
    y = y.astype(x.dtype)
    return y if g is None else y * g


def _lambda_init(layer_idx):
    return 0.8 - 0.6 * math.exp(-0.3 * (layer_idx - 1))


def _t5_bucket(rel):
    nb = NUM_BUCKETS // 2
    max_exact = nb // 2
    ret = jnp.where(rel > 0, nb, 0)
    n = jnp.abs(rel)
    nf = jnp.maximum(n, 1).astype(jnp.float32)
    large = max_exact + (jnp.log(nf / max_exact) / math.log(MAX_DISTANCE / max_exact)
                         * (nb - max_exact)).astype(jnp.int32)
    large = jnp.minimum(large, nb - 1)
    return ret + jnp.where(n < max_exact, n, large)


def _rope(x, cos, sin):
    half = x.shape[-1] // 2
    x1, x2 = x[..., :half], x[..., half:]
    out = jnp.concatenate([x1 * cos - x2 * sin, x2 * cos + x1 * sin], axis=-1)
    return out.astype(x.dtype)


def _diff_attention(q, k, v, positions, bias_table, lam, subln_g, lambda_init):
    B, S, H, _, d = q.shape
    dv = v.shape[-1]
    nblk = S // Q_BLOCK
    scale = 1.0 / math.sqrt(d)
    qb = q.reshape(B, nblk, Q_BLOCK, H, 2, d).transpose(1, 0, 3, 4, 2, 5)
    kt = k.transpose(0, 2, 3, 1, 4)
    vt = v.transpose(0, 2, 1, 3)
    pb = positions.reshape(B, nblk, Q_BLOCK).transpose(1, 0, 2)

    def block(args):
        qi, pi = args
        s = jnp.einsum('bhmqd,bhmkd->bhmqk', qi, kt).astype(jnp.float32) * scale
        rel = positions[:, None, :] - pi[:, :, None]
        bias = bias_table[_t5_bucket(rel)].astype(jnp.float32)
        bias = bias.transpose(0, 3, 1, 2)[:, :, None]
        p = jax.nn.softmax(s + bias, axis=-1)
        a = p[:, :, 0] - lam * p[:, :, 1]
        return jnp.einsum('bhqk,bhkd->bhqd', a.astype(vt.dtype), vt)

    o = lax.map(block, (qb, pb))
    o = o.transpose(1, 0, 3, 2, 4).reshape(B, S, H, dv)
    o = _rmsnorm(o, subln_g) * (1.0 - lambda_init)
    return o.reshape(B, S, H * dv)


def _mla_attention(q_nope, q_rope, k_nope, k_rope, v):
    B, S, H, dn = q_nope.shape
    dr = q_rope.shape[-1]
    dv = v.shape[-1]
    nblk = S // Q_BLOCK
    scale = 1.0 / math.sqrt(dn + dr)
    qnb = q_nope.reshape(B, nblk, Q_BLOCK, H, dn).transpose(1, 0, 3, 2, 4)
    qrb = q_rope.reshape(B, nblk, Q_BLOCK, H, dr).transpose(1, 0, 3, 2, 4)
    knt = k_nope.transpose(0, 2, 1, 3)
    vt = v.transpose(0, 2, 1, 3)

    def block(args):
        qn, qr = args
        s = (jnp.einsum('bhqd,bhkd->bhqk', qn, knt)
             + jnp.einsum('bhqr,bkr->bhqk', qr, k_rope)).astype(jnp.float32) * scale
        p = jax.nn.softmax(s, axis=-1)
        return jnp.einsum('bhqk,bhkd->bhqd', p.astype(vt.dtype), vt)

    o = lax.map(block, (qnb, qrb))
    return o.transpose(1, 0, 3, 2, 4).reshape(B, S, H * dv)


def setup_inputs(seed: int = 0) -> dict:
    key = jax.random.key(seed)
    ks = jax.random.split(key, 24)
    f32 = jnp.float32

    def nrm(k, shape, scale):
        return jax.random.normal(k, shape, f32) * scale

    x = jax.random.normal(ks[0], (BATCH, SEQ, D_MODEL), f32)
    c = jax.random.normal(ks[1], (BATCH, D_MODEL), f32)
    offset = jax.random.randint(ks[2], (BATCH,), 0, 1024, dtype=jnp.int32)
    positions = (offset[:, None] + jnp.arange(SEQ, dtype=jnp.int32)[None, :]).astype(jnp.int32)
    return {
        "x": x,
        "c": c,
        "positions": positions,
        "rel_bias_table": nrm(ks[3], (NUM_BUCKETS, DIFF_HEADS), 0.5),
        "w_ada": nrm(ks[4], (DEPTH, D_MODEL, 6 * D_MODEL), D_MODEL ** -0.5),
        "b_ada": nrm(ks[5], (DEPTH, 6 * D_MODEL), 0.02),
        "w_in": nrm(ks[6], (DEPTH, D_MODEL, IN_COLS), D_MODEL ** -0.5),
        "diff_lambda_q1": nrm(ks[7], (DEPTH, DIFF_HEAD_DIM), 0.1),
        "diff_lambda_k1": nrm(ks[8], (DEPTH, DIFF_HEAD_DIM), 0.1),
        "diff_lambda_q2": nrm(ks[9], (DEPTH, DIFF_HEAD_DIM), 0.1),
        "diff_lambda_k2": nrm(ks[10], (DEPTH, DIFF_HEAD_DIM), 0.1),
        "diff_subln_g": 1.0 + nrm(ks[11], (DEPTH, DIFF_V_DIM), 0.02),
        "q_norm_g": 1.0 + nrm(ks[12], (DEPTH, Q_LORA), 0.02),
        "w_uq": nrm(ks[13], (DEPTH, Q_LORA, MLA_HEADS * (MLA_NOPE + MLA_ROPE)), Q_LORA ** -0.5),
        "kv_norm_g": 1.0 + nrm(ks[14], (DEPTH, KV_LORA), 0.02),
        "w_ukv": nrm(ks[15], (DEPTH, KV_LORA, MLA_HEADS * (MLA_NOPE + MLA_V)), KV_LORA ** -0.5),
        "w_o": nrm(ks[16], (DEPTH, D_MIX, D_MODEL), D_MIX ** -0.5),
        "w_mlp_in": nrm(ks[17], (DEPTH, D_MODEL, D_FF), D_MODEL ** -0.5),
        "w_mlp_out": nrm(ks[18], (DEPTH, D_FF, D_MODEL), D_FF ** -0.5),
        "final_norm_g": 1.0 + nrm(ks[19], (D_MODEL,), 0.02),
    }


def reference(x, c, positions, rel_bias_table, w_ada, b_ada, w_in,
              diff_lambda_q1, diff_lambda_k1, diff_lambda_q2, diff_lambda_k2,
              diff_subln_g, q_norm_g, w_uq, kv_norm_g, w_ukv, w_o,
              w_mlp_in, w_mlp_out, final_norm_g):
    B, S, D = x.shape
    inv_freq = 1.0 / (ROPE_THETA ** (jnp.arange(0, MLA_ROPE, 2, dtype=jnp.float32) / MLA_ROPE))
    ang = positions.astype(jnp.float32)[..., None] * inv_freq
    cos, sin = jnp.cos(ang), jnp.sin(ang)
    cond = jax.nn.silu(c)

    for l in range(DEPTH):
        lambda_init = _lambda_init(l + 1)
        mod = cond @ w_ada[l] + b_ada[l]
        sh_a, sc_a, g_a, sh_m, sc_m, g_m = [m[:, None, :] for m in jnp.split(mod, 6, axis=-1)]

        h = _rmsnorm(x) * (1.0 + sc_a) + sh_a
        proj = h @ w_in[l]
        dq = proj[..., OFF_DQ:OFF_DK].reshape(B, S, DIFF_HEADS, 2, DIFF_HEAD_DIM)
        dk = proj[..., OFF_DK:OFF_DV].reshape(B, S, DIFF_HEADS, 2, DIFF_HEAD_DIM)
        dv = proj[..., OFF_DV:OFF_CQ].reshape(B, S, DIFF_HEADS, DIFF_V_DIM)
        cq = proj[..., OFF_CQ:OFF_CKV]
        ckv = proj[..., OFF_CKV:OFF_KR]
        kr = proj[..., OFF_KR:IN_COLS]

        lam = (jnp.exp(jnp.sum(diff_lambda_q1[l] * diff_lambda_k1[l]))
               - jnp.exp(jnp.sum(diff_lambda_q2[l] * diff_lambda_k2[l])) + lambda_init)
        out_a = _diff_attention(dq, dk, dv, positions, rel_bias_table, lam,
                                diff_subln_g[l], lambda_init)

        q = (_rmsnorm(cq, q_norm_g[l]) @ w_uq[l]).reshape(B, S, MLA_HEADS, MLA_NOPE + MLA_ROPE)
        q_nope = q[..., :MLA_NOPE]
        q_rope = _rope(q[..., MLA_NOPE:], cos[:, :, None, :], sin[:, :, None, :])
        kv = (_rmsnorm(ckv, kv_norm_g[l]) @ w_ukv[l]).reshape(B, S, MLA_HEADS, MLA_NOPE + MLA_V)
        k_nope, v_m = kv[..., :MLA_NOPE], kv[..., MLA_NOPE:]
        k_rope = _rope(kr, cos, sin)
        out_b = _mla_attention(q_nope, q_rope, k_nope, k_rope, v_m)

        mix = jnp.concatenate([out_a, out_b], axis=-1) @ w_o[l]
        x = x + g_a * mix

        h = _rmsnorm(x) * (1.0 + sc_m) + sh_m
        y = jnp.square(jax.nn.relu(h @ w_mlp_in[l])) @ w_mlp_out[l]
        x = x + g_m * y

    return _rmsnorm(x, final_norm_g)
```

```python
import contextlib
import math
import numpy as np
import ml_dtypes
import concourse.bass as bass
import concourse.mybir as mybir
from concourse.bass_utils import run_bass_kernel_spmd

F32 = mybir.dt.float32
BF16 = mybir.dt.bfloat16
I32 = mybir.dt.int32
ALU = mybir.AluOpType
AF = mybir.ActivationFunctionType

D = 2048
DFF = 8192
NH_D = 4
NH_M = 8
OFF_DQ, OFF_DK, OFF_DV, OFF_CQ, OFF_CKV, OFF_KR = 0, 1024, 2048, 3072, 3584, 3840
IN_COLS = 3904
WIN_COLS = IN_COLS + 64
WUQ_COLS = 1536 + 512
EPS = 1e-6
THRESH = [1, 2, 3, 4, 5, 6, 7, 8, 12, 16, 23, 32, 46, 64, 91]
MAGIC = 12582912.0
TWO_PI = 2.0 * math.pi
C1 = 6.28125
C2 = TWO_PI - C1
ENGS = ("pe", "act", "dve", "pool", "sp")


class Buf:
    __slots__ = ("name", "w", "r", "dsem", "dcnt")

    def __init__(self, name, dsem=None):
        self.name = name
        self.w = {}
        self.r = {}
        self.dsem = dsem
        self.dcnt = 0


class Prog:
    def __init__(self, nc, es):
        self.nc = nc
        self.es = es
        self.q = {e: [] for e in ENGS}
        self.cnt = {e: 0 for e in ENGS}
        self.esem = {}
        self.bufs = []
        self.nsem = 0
        for e in ("pe", "act", "dve", "pool"):
            self.esem[e] = self.new_sem("e_" + e)

    def new_sem(self, name):
        self.nsem += 1
        return self.es.enter_context(self.nc.semaphore(name))

    def buf(self, name, dma=False):
        b = Buf(name, self.new_sem("d_" + name) if dma else None)
        self.bufs.append(b)
        return b

    def _waits(self, eng, reads, writes):
        waits = {}

        def need(d):
            for k, v in d.items():
                if waits.get(k, (0,))[0] < v[0]:
                    waits[k] = v

        for b in reads:
            need(b.w)
        for b in writes:
            need(b.w)
            need(b.r)
        if eng == "pe":
            waits.pop(id(self.esem["pe"]), None)
        return list(waits.values())

    @staticmethod
    def _mark(reads, writes, key, t, s):
        for b in reads:
            if b.r.get(key, (0,))[0] < t:
                b.r[key] = (t, s)
        for b in writes:
            b.w = {key: (t, s)}
            b.r = {}

    def op(self, eng, fn, reads=(), writes=()):
        waits = self._waits(eng, reads, writes)
        self.cnt[eng] += 1
        t = self.cnt[eng]
        s = self.esem[eng]
        self.q[eng].append((waits, fn, s, 1))
        self._mark(reads, writes, id(s), t, s)

    def dma(self, qeng, pieces, reads, writes, sbuf):
        waits = self._waits(qeng, reads, writes)
        s = sbuf.dsem
        first = True
        for (o, i) in pieces:
            sbuf.dcnt += 16

            def fn(e, o=o, i=i):
                return e.dma_start(out=o, in_=i)

            self.q[qeng].append((waits if first else [], fn, s, 16))
            first = False
        self._mark(reads, writes, id(s), sbuf.dcnt, s)

    def barrier(self):
        waits = {}
        for b in self.bufs:
            for d in (b.w, b.r):
                for k, v in d.items():
                    if waits.get(k, (0,))[0] < v[0]:
                        waits[k] = v
        wl = list(waits.values())
        for e in ENGS:
            self.q[e].append((wl, None, None, 0))

    def emit(self, block):
        engmap = {"pe": block.tensor, "act": block.scalar, "dve": block.vector,
                  "pool": block.gpsimd, "sp": block.sync}
        for ename in ENGS:
            items = self.q[ename]

            def body(e, items=items):
                waited = {}
                for (waits, fn, s, inc) in items:
                    for (v, ws) in waits:
                        k = id(ws)
                        if waited.get(k, 0) < v:
                            e.wait_ge(ws, v)
                            waited[k] = v
                    if fn is not None:
                        fn(e).then_inc(s, inc)

            engmap[ename](body)


class Ctx:
    def __init__(self, nc, es):
        self.nc = nc
        self.es = es
        self.P = Prog(nc, es)
        self.uid = 0

    def sb(self, st, shape, dt, name):
        self.uid += 1
        return st.enter_context(self.nc.sbuf_tensor(f"{name}_{self.uid}", list(shape), dt))

    def ps(self, st, shape, dt, name):
        self.uid += 1
        return st.enter_context(self.nc.psum_tensor(f"{name}_{self.uid}", list(shape), dt))


def mm(P, out, lhsT, rhs, start, stop, reads, writes):
    P.op("pe", lambda e: e.matmul(out, lhsT, rhs, start=start, stop=stop), reads, writes)


def rms_rstd(C, ss_ap, rstd_ap, nfeat, ssb, rsb, epsb):
    P = C.P
    P.op("act", lambda e: e.activation(out=rstd_ap, in_=ss_ap, func=AF.Sqrt,
                                       bias=epsb[0][0:ss_ap.shape[0], :], scale=1.0 / nfeat),
         [ssb, epsb[1]], [rsb])
    P.op("dve", lambda e: e.reciprocal(out=rstd_ap, in_=rstd_ap), [rsb], [rsb])


def transpose_blocks(C, src_fn, nk, dstT, dst_buf, src_buf, tb, ptb, ident, identb, cp_eng):
    P = C.P
    k = 0
    gi = 0
    while k < nk:
        n = min(8, nk - k)
        pt, ptbuf = ptb[(tb + gi) % len(ptb)]
        for j in range(n):
            P.op("pe", lambda e, j=j, k=k, pt=pt: e.transpose(pt[:, j * 128:(j + 1) * 128], src_fn(k + j), ident),
                 [src_buf, identb], [ptbuf])
        o = dstT[:, k:k + n, tb * 128:(tb + 1) * 128]
        i = pt[:, 0:n * 128].rearrange("p (k t) -> p k t", t=128)
        eng = cp_eng[(tb + gi) % len(cp_eng)]
        if eng == "act":
            P.op("act", lambda e, o=o, i=i: e.copy(out=o, in_=i), [ptbuf], [dst_buf])
        else:
            P.op("dve", lambda e, o=o, i=i: e.tensor_copy(out=o, in_=i), [ptbuf], [dst_buf])
        k += n
        gi += 1


def phase0(C, dr, NW, NS, depth):
    P = C.P
    with contextlib.ExitStack() as st:
        stg = [(C.sb(st, [128, 2048], F32, "wst"), P.buf(f"wst{i}", dma=True)) for i in range(3)]
        obf = [(C.sb(st, [128, 2048], BF16, "wob"), P.buf(f"wob{i}", dma=True)) for i in range(3)]
        sgn = C.sb(st, [128, 512], F32, "sgn")
        sgnb = P.buf("sgn", dma=True)
        P.dma("sp", [(sgn[:], dr["sgn"][:, :])], [], [sgnb], sgnb)
        jobs = []
        for c0 in range(0, NS, 512):
            cw = min(512, NS - c0)
            jobs.append((dr["wsig"][:, c0:c0 + cw], dr["wbsig"][:, c0:c0 + cw], cw, True))
        for c0 in range(0, NW, 2048):
            cw = min(2048, NW - c0)
            jobs.append((dr["wflat"][:, c0:c0 + cw], dr["wbflat"][:, c0:c0 + cw], cw, False))
        engs = ["dve", "pool", "act"]
        for it, (s_ap, d_ap, cw, signed) in enumerate(jobs):
            sl = it % 3
            (st_t, st_b), (ob_t, ob_b) = stg[sl], obf[sl]
            P.dma("sp", [(st_t[:, 0:cw], s_ap)], [], [st_b], st_b)
            if signed:
                P.op("dve", lambda e, a=ob_t[:, 0:cw], b=st_t[:, 0:cw], c=sgn[:, 0:cw]:
                     e.tensor_tensor(out=a, in0=b, in1=c, op=ALU.mult), [st_b, sgnb], [ob_b])
            else:
                eng = engs[it % 3]
                if eng == "act":
                    P.op("act", lambda e, a=ob_t[:, 0:cw], b=st_t[:, 0:cw]: e.copy(out=a, in_=b), [st_b], [ob_b])
                else:
                    P.op(eng, lambda e, a=ob_t[:, 0:cw], b=st_t[:, 0:cw]: e.tensor_copy(out=a, in_=b), [st_b], [ob_b])
            P.dma("sp", [(d_ap, ob_t[:, 0:cw])], [ob_b], [], ob_b)
        cT = C.sb(st, [128, 16, 2], F32, "cT2"); cTb = P.buf("cT2", dma=True)
        P.dma("sp", [(cT[:], dr["cT2"][:, :, :])], [], [cTb], cTb)
        P.op("act", lambda e: e.activation(out=cT[:], in_=cT[:], func=AF.Silu), [cTb], [cTb])
        mo = C.sb(st, [2, depth, 1536], F32, "mo"); mob = P.buf("mo", dma=True)
        P.dma("sp", [(mo[:], dr["bsl"][:, :, :])], [], [mob], mob)
        wa = [(C.sb(st, [128, 4, 512], F32, "wa"), P.buf(f"wa{i}", dma=True)) for i in range(3)]
        pm = [(C.ps(st, [128, 512], F32, "pm"), P.buf(f"pm{i}")) for i in range(2)]
        it = 0
        for l in range(depth):
            wav = dr["wada"][l].rearrange("(kc p) n -> p kc n", p=128)
            for cg in range(3):
                pmt, pmb = pm[cg % 2]
                for kg in range(4):
                    wt, wb = wa[it % 3]
                    it += 1
                    P.dma("sp", [(wt[:], wav[:, kg * 4:(kg + 1) * 4, cg * 512:(cg + 1) * 512])], [], [wb], wb)
                    for k4 in range(4):
                        kc = kg * 4 + k4
                        mm(P, pmt[0:2, :], cT[:, kc, :], wt[:, k4, :], kc == 0, kc == 15, [cTb, wb], [pmb])
                sl = mo[:, l, cg * 512:(cg + 1) * 512]
                P.op("dve", lambda e, sl=sl, pmt=pmt: e.tensor_tensor(out=sl, in0=pmt[0:2, :], in1=sl, op=ALU.add), [pmb, mob], [mob])
        P.dma("sp", [(dr["mod_out"][:, :, :], mo[:])], [mob], [], mob)
        P.barrier()


def load_mod(C, dr, c0, ncols, modt, modb, plus_one_ranges):
    P = C.P
    P.dma("sp", [(modt[:, 0:ncols], dr["modv"][c0:c0 + ncols].partition_broadcast(128))], [], [modb], modb)
    for (a, b) in plus_one_ranges:
        P.op("dve", lambda e, a=a, b=b: e.tensor_scalar(out=modt[:, a:b], in0=modt[:, a:b], scalar1=1.0, scalar2=None,
                                                        op0=ALU.add), [modb], [modb])


def norm_mod_transpose(C, xt, xb, nb, sh_ap, sc_ap, modb, hb_t, hbb, hT, hTb, tmpf, tmpfb, junk, junkb,
                       ss, ssb, rstd, rsb, epsb, ptb, ident, identb):
    P = C.P
    for tb in range(nb):
        P.op("act", lambda e, tb=tb: e.activation(out=junk[:], in_=xt[:, tb, :], func=AF.Square,
                                                 accum_out=ss[:, tb:tb + 1]), [xb], [junkb, ssb])
    rms_rstd(C, ss[:, 0:nb], rstd[:, 0:nb], D, ssb, rsb, epsb)
    for tb in range(nb):
        P.op("dve", lambda e, tb=tb: e.scalar_tensor_tensor(out=tmpf[:], in0=xt[:, tb, :], scalar=rstd[:, tb:tb + 1],
                                                           in1=sc_ap, op0=ALU.mult, op1=ALU.mult),
             [xb, rsb, modb], [tmpfb])
        P.op("pool", lambda e, tb=tb: e.tensor_tensor(out=hb_t[:, tb, :], in0=tmpf[:], in1=sh_ap, op=ALU.add),
             [tmpfb, modb], [hbb])
        transpose_blocks(C, lambda k, tb=tb: hb_t[:, tb, k * 128:(k + 1) * 128], 16, hT, hTb, hbb, tb, ptb,
                         ident, identb, ["act", "dve"])


def make_ident(C, st):
    P = C.P
    identf = C.sb(st, [128, 128], F32, "identf")
    ident = C.sb(st, [128, 128], BF16, "ident")
    identb = P.buf("ident")
    P.op("pool", lambda e: e.memset(identf[:], 0.0), [], [identb])
    P.op("pool", lambda e: e.affine_select(out=identf[:], in_=identf[:], pattern=[[-1, 128]], compare_op=ALU.not_equal,
                                           fill=1.0, base=0, channel_multiplier=1), [identb], [identb])
    P.op("dve", lambda e: e.tensor_copy(out=ident[:], in_=identf[:]), [identb], [identb])
    return ident[:], identb


def make_eps(C, st):
    P = C.P
    t = C.sb(st, [128, 1], F32, "epsc")
    b = P.buf("epsc")
    P.op("dve", lambda e: e.memset(t[:], EPS), [], [b])
    return (t, b)


def phase_a(C, dr, T):
    P = C.P
    TT = min(512, T)
    NB = TT // 128
    with contextlib.ExitStack() as st:
        modt = C.sb(st, [128, 4096], F32, "modA")
        modb = P.buf("modA", dma=True)
        load_mod(C, dr, 0, 4096, modt, modb, [(2048, 4096)])
        sh_ap, sc_ap = modt[:, 0:2048], modt[:, 2048:4096]
        ident, identb = make_ident(C, st)
        epsb = make_eps(C, st)
        xt = C.sb(st, [128, NB, 2048], F32, "xt"); xb = P.buf("xt", dma=True)
        tmpf = C.sb(st, [128, 2048], F32, "tmpf"); tmpfb = P.buf("tmpf")
        junk = C.sb(st, [128, 2048], BF16, "junk"); junkb = P.buf("junk")
        hb_t = C.sb(st, [128, NB, 2048], BF16, "hb"); hbb = P.buf("hb")
        hT = C.sb(st, [128, 16, TT], BF16, "hT"); hTb = P.buf("hT")
        wg = [(C.sb(st, [128, 16, 512], BF16, "wg"), P.buf(f"wg{i}", dma=True)) for i in range(2)]
        wuq = C.sb(st, [128, 4, WUQ_COLS], BF16, "wuq"); wuqb = P.buf("wuq", dma=True)
        wukv = C.sb(st, [128, 2, 2048], BF16, "wukv"); wukvb = P.buf("wukv", dma=True)
        gq = C.sb(st, [128, 512], F32, "gq"); gkv = C.sb(st, [128, 256], F32, "gkv"); gb = P.buf("gqkv", dma=True)
        cqT = C.sb(st, [128, 4, TT], BF16, "cqT"); cqTb = P.buf("cqT")
        ckvT = C.sb(st, [128, 2, TT], BF16, "ckvT"); ckvTb = P.buf("ckvT")
        cn = C.sb(st, [128, 512], BF16, "cn"); cnb = P.buf("cn")
        qst = [(C.sb(st, [128, 4, TT], BF16, "qst"), P.buf(f"qst{i}", dma=True)) for i in range(2)]
        rst = [(C.sb(st, [64, TT], BF16, "rst"), P.buf(f"rst{i}", dma=True)) for i in range(2)]
        vst = C.sb(st, [128, 4, NB, 258], BF16, "vst"); vstb = P.buf("vst", dma=True)
        vmst = C.sb(st, [128, 8, NB, 130], BF16, "vmst"); vmstb = P.buf("vmst", dma=True)
        ss = C.sb(st, [128, 8], F32, "ss"); ssb = P.buf("ss")
        rstd = C.sb(st, [128, 8], F32, "rstd"); rsb = P.buf("rstd")
        posi = C.sb(st, [64, TT], I32, "posi"); posib = P.buf("posi", dma=True)
        invf = C.sb(st, [64, 1], F32, "invf"); invfb = P.buf("invf", dma=True)
        halfpi = C.sb(st, [64, 1], F32, "halfpi")
        ang = C.sb(st, [64, TT], F32, "ang"); kk = C.sb(st, [64, TT], F32, "kk"); rr = C.sb(st, [64, TT], F32, "rr")
        trb = P.buf("trig")
        cosT = C.sb(st, [64, TT], F32, "cosT"); sinT = C.sb(st, [64, TT], F32, "sinT"); csb = P.buf("cossin")
        ra = C.sb(st, [64, TT], F32, "ra"); rb_ = C.sb(st, [64, TT], F32, "rb"); rab = P.buf("ra")
        ps = [(C.ps(st, [128, 512], F32, "psA"), P.buf(f"psA{i}")) for i in range(6)]
        ptb = [(C.ps(st, [128, 1024], BF16, "ptA"), P.buf(f"ptA{i}")) for i in range(2)]

        P.dma("sp", [(wuq[:], dr["wb_uq"].rearrange("(kc p) n -> p kc n", p=128))], [], [wuqb], wuqb)
        P.dma("sp", [(wukv[:], dr["wb_ukv"].rearrange("(kc p) n -> p kc n", p=128))], [], [wukvb], wukvb)
        P.dma("sp", [(gq[:], dr["q_norm_g"][:].partition_broadcast(128)),
                     (gkv[:], dr["kv_norm_g"][:].partition_broadcast(128))], [], [gb], gb)
        P.dma("sp", [(invf[:], dr["invf"][:, :])], [], [invfb], invfb)
        P.op("pool", lambda e: e.memset(halfpi[:], math.pi / 2), [], [trb])
        P.op("pool", lambda e: e.memset(vst[:], 0.0), [], [vstb])
        P.op("pool", lambda e: e.memset(vst[:, :, :, 256:257], 1.0), [vstb], [vstb])
        P.op("pool", lambda e: e.memset(vmst[:], 0.0), [], [vmstb])
        P.op("pool", lambda e: e.memset(vmst[:, :, :, 128:129], 1.0), [vmstb], [vmstb])

        win = dr["wb_in"].rearrange("(kc p) n -> p kc n", p=128)
        pcount = [0]

        def nps():
            pcount[0] += 1
            return ps[pcount[0] % 6]

        ecnt = [0]

        def evac_copy(o, i, reads, writes):
            ecnt[0] += 1
            if ecnt[0] % 2:
                P.op("act", lambda e: e.copy(out=o, in_=i), reads, writes)
            else:
                P.op("dve", lambda e: e.tensor_copy(out=o, in_=i), reads, writes)

        def rope_evac(p0, p1, b0, b1, dst, dstb):
            P.op("dve", lambda e: e.tensor_tensor(out=ra[:], in0=p0, in1=cosT[:], op=ALU.mult), [b0, csb], [rab])
            P.op("dve", lambda e: e.tensor_tensor(out=rb_[:], in0=p1, in1=sinT[:], op=ALU.mult), [b1, csb], [rab])
            P.op("pool", lambda e: e.tensor_tensor(out=dst, in0=ra[:], in1=rb_[:], op=ALU.add), [rab], [dstb])

        def tok_norm_T(ps_ap, pb_, n, g_ap, dstT, dstb, tb):
            P.op("act", lambda e: e.activation(out=junk[:, 0:n], in_=ps_ap, func=AF.Square, accum_out=ss[:, 7:8]),
                 [pb_], [junkb, ssb])
            rms_rstd(C, ss[:, 7:8], rstd[:, 7:8], n, ssb, rsb, epsb)
            P.op("dve", lambda e: e.scalar_tensor_tensor(out=cn[:, 0:n], in0=ps_ap, scalar=rstd[:, 7:8], in1=g_ap,
                                                        op0=ALU.mult, op1=ALU.mult), [pb_, rsb, gb], [cnb])
            transpose_blocks(C, lambda k: cn[:, k * 128:(k + 1) * 128], n // 128, dstT, dstb, cnb, tb, ptb,
                             ident, identb, ["act", "dve"])

        qsi = [0]
        rsi = [0]
        wgi = [0]

        def load_wg(c0, ncols):
            wt, wb = wg[wgi[0] % 2]
            wgi[0] += 1
            P.dma("sp", [(wt[:, :, 0:ncols], win[:, :, c0:c0 + ncols])], [], [wb], wb)
            return wt, wb

        def next_q():
            r = qst[qsi[0] % 2]
            qsi[0] += 1
            return r

        def next_r():
            r = rst[rsi[0] % 2]
            rsi[0] += 1
            return r

        for tt in range(T // TT):
            t0 = tt * TT
            P.dma("sp", [(xt[:], dr["x"][t0:t0 + TT, :].rearrange("(b p) d -> p b d", p=128))], [], [xb], xb)
            P.dma("sp", [(posi[:], dr["pos"][0, t0:t0 + TT].partition_broadcast(64))], [], [posib], posib)
            P.op("dve", lambda e: e.tensor_copy(out=rr[:], in_=posi[:]), [posib], [trb])
            P.op("dve", lambda e: e.tensor_scalar(out=ang[:], in0=rr[:], scalar1=invf[:, 0:1], scalar2=None, op0=ALU.mult),
                 [trb, invfb], [trb])
            for (shift, dst, bias_ap) in ((0.0, sinT, None), (0.25, cosT, halfpi)):
                P.op("dve", lambda e, shift=shift: e.tensor_scalar(out=kk[:], in0=ang[:], scalar1=1.0 / TWO_PI, scalar2=shift,
                                                                   op0=ALU.mult, op1=ALU.add), [trb], [trb])
                P.op("dve", lambda e: e.tensor_scalar(out=kk[:], in0=kk[:], scalar1=MAGIC, scalar2=None, op0=ALU.add), [trb], [trb])
                P.op("dve", lambda e: e.tensor_scalar(out=kk[:], in0=kk[:], scalar1=-MAGIC, scalar2=None, op0=ALU.add), [trb], [trb])
                P.op("dve", lambda e: e.scalar_tensor_tensor(out=rr[:], in0=kk[:], scalar=-C1, in1=ang[:], op0=ALU.mult, op1=ALU.add),
                     [trb], [trb])
                P.op("dve", lambda e: e.scalar_tensor_tensor(out=rr[:], in0=kk[:], scalar=-C2, in1=rr[:], op0=ALU.mult, op1=ALU.add),
                     [trb], [trb])
                if bias_ap is None:
                    P.op("act", lambda e, dst=dst: e.activation(out=dst[:], in_=rr[:], func=AF.Sin), [trb], [csb])
                else:
                    P.op("act", lambda e, dst=dst, bias_ap=bias_ap: e.activation(out=dst[:], in_=rr[:], func=AF.Sin, bias=bias_ap[:]),
                         [trb], [csb])
            norm_mod_transpose(C, xt, xb, NB, sh_ap, sc_ap, modb, hb_t, hbb, hT, hTb, tmpf, tmpfb, junk, junkb,
                               ss, ssb, rstd, rsb, epsb, ptb, ident, identb)
            for gi, (c0, dkey) in enumerate(((OFF_DQ, "qT_d"), (OFF_DQ + 512, "qT_d"), (OFF_DK, "kT_d"), (OFF_DK + 512, "kT_d"))):
                wt, wb = load_wg(c0, 512)
                qt, qb_ = next_q()
                for oc in range(4):
                    pt_, pb_ = nps()
                    for kc in range(16):
                        mm(P, pt_[:, 0:TT], wt[:, kc, oc * 128:(oc + 1) * 128], hT[:, kc, :], kc == 0, kc == 15, [wb, hTb], [pb_])
                    evac_copy(qt[:, oc, :], pt_[:, 0:TT], [pb_], [qb_])
                m0 = (gi % 2) * 4
                P.dma("sp", [(dr[dkey][m0:m0 + 4, :, t0:t0 + TT].rearrange("m p t -> p m t"), qt[:])], [qb_], [], qb_)
            for g in range(2):
                wt, wb = load_wg(OFF_DV + g * 512, 512)
                for tb in range(NB):
                    pt_, pb_ = nps()
                    for kc in range(16):
                        mm(P, pt_[:], hT[:, kc, tb * 128:(tb + 1) * 128], wt[:, kc, :], kc == 0, kc == 15, [wb, hTb], [pb_])
                    evac_copy(vst[:, 2 * g:2 * g + 2, tb, 0:256], pt_[:].rearrange("p (h c) -> p h c", c=256), [pb_], [vstb])
            P.dma("sp", [(dr["v_d"][:, :, tt * NB:(tt + 1) * NB, :].rearrange("h p t c -> p h (t c)"),
                          vst[:].rearrange("p h t c -> p h (t c)"))], [vstb], [], vstb)
            wt, wb = load_wg(OFF_CQ, 512)
            for tb in range(NB):
                pt_, pb_ = nps()
                for kc in range(16):
                    mm(P, pt_[:], hT[:, kc, tb * 128:(tb + 1) * 128], wt[:, kc, :], kc == 0, kc == 15, [wb, hTb], [pb_])
                tok_norm_T(pt_[:], pb_, 512, gq[:], cqT, cqTb, tb)
            wt, wb = load_wg(OFF_CKV, 384)
            for tb in range(NB):
                pt_, pb_ = nps()
                for kc in range(16):
                    mm(P, pt_[:, 0:256], hT[:, kc, tb * 128:(tb + 1) * 128], wt[:, kc, 0:256], kc == 0, kc == 15, [wb, hTb], [pb_])
                tok_norm_T(pt_[:, 0:256], pb_, 256, gkv[:], ckvT, ckvTb, tb)
            pkr, pkrb = nps()
            for kc in range(16):
                mm(P, pkr[0:64, 0:TT], wt[:, kc, 256:320], hT[:, kc, :], kc == 0, kc == 15, [wb, hTb], [pkrb])
            pkq, pkqb = nps()
            for kc in range(16):
                mm(P, pkq[0:64, 0:TT], wt[:, kc, 320:384], hT[:, kc, :], kc == 0, kc == 15, [wb, hTb], [pkqb])
            rt, rtb = next_r()
            rope_evac(pkr[0:64, 0:TT], pkq[0:64, 0:TT], pkrb, pkqb, rt[:], rtb)
            P.dma("sp", [(dr["kT_r"][:, t0:t0 + TT], rt[:])], [rtb], [], rtb)
            for g in range(2):
                qt, qb_ = next_q()
                for hh in range(4):
                    h = g * 4 + hh
                    pt_, pb_ = nps()
                    for kc in range(4):
                        mm(P, pt_[:, 0:TT], wuq[:, kc, 192 * h:192 * h + 128], cqT[:, kc, :], kc == 0, kc == 3, [wuqb, cqTb], [pb_])
                    evac_copy(qt[:, hh, :], pt_[:, 0:TT], [pb_], [qb_])
                P.dma("sp", [(dr["qT_n"][g * 4:g * 4 + 4, :, t0:t0 + TT].rearrange("m p t -> p m t"), qt[:])], [qb_], [], qb_)
            for h in range(8):
                p0, b0 = nps()
                for kc in range(4):
                    mm(P, p0[0:64, 0:TT], wuq[:, kc, 192 * h + 128:192 * h + 192], cqT[:, kc, :], kc == 0, kc == 3, [wuqb, cqTb], [b0])
                p1, b1 = nps()
                for kc in range(4):
                    mm(P, p1[0:64, 0:TT], wuq[:, kc, 1536 + 64 * h:1536 + 64 * h + 64], cqT[:, kc, :], kc == 0, kc == 3,
                       [wuqb, cqTb], [b1])
                rt, rtb = next_r()
                rope_evac(p0[0:64, 0:TT], p1[0:64, 0:TT], b0, b1, rt[:], rtb)
                P.dma("sp", [(dr["qT_r"][h, :, t0:t0 + TT], rt[:])], [rtb], [], rtb)
            for g in range(2):
                qt, qb_ = next_q()
                for hh in range(4):
                    h = g * 4 + hh
                    pt_, pb_ = nps()
                    for kc in range(2):
                        mm(P, pt_[:, 0:TT], wukv[:, kc, 256 * h:256 * h + 128], ckvT[:, kc, :], kc == 0, kc == 1, [wukvb, ckvTb], [pb_])
                    evac_copy(qt[:, hh, :], pt_[:, 0:TT], [pb_], [qb_])
                P.dma("sp", [(dr["kT_n"][g * 4:g * 4 + 4, :, t0:t0 + TT].rearrange("m p t -> p m t"), qt[:])], [qb_], [], qb_)
            for tb in range(NB):
                for g in range(2):
                    pt_, pb_ = nps()
                    for kc in range(2):
                        rhs = wukv[:, kc, :].rearrange("p (h c) -> p h c", c=256)[:, 4 * g:4 * g + 4, 128:256]
                        mm(P, pt_[:].rearrange("p (h c) -> p h c", c=128), ckvT[:, kc, tb * 128:(tb + 1) * 128], rhs,
                           kc == 0, kc == 1, [wukvb, ckvTb], [pb_])
                    evac_copy(vmst[:, 4 * g:4 * g + 4, tb, 0:128], pt_[:].rearrange("p (h c) -> p h c", c=128), [pb_], [vmstb])
            P.dma("sp", [(dr["v_m"][:, :, tt * NB:(tt + 1) * NB, :].rearrange("h p t c -> p h (t c)"),
                          vmst[:].rearrange("p h t c -> p h (t c)"))], [vmstb], [], vmstb)
        P.barrier()
def bias_eval(C, rel, relb, W, outs, tb, dpos, dneg, tabb, mask, maskb):
    P = C.P
    engs = ["dve", "dve", "dve", "dve"]
    for h in range(4):
        o, ob = outs[h]
        P.op("dve", lambda e, o=o, h=h: e.tensor_scalar(out=o[:, 0:W], in0=rel, scalar1=1.0, scalar2=dpos[:, 4 + h:5 + h],
                                                        op0=ALU.is_ge, op1=ALU.mult), [relb, tabb], [ob])
        P.op("dve", lambda e, o=o, h=h: e.tensor_scalar(out=o[:, 0:W], in0=o[:, 0:W], scalar1=tb[:, h:h + 1], scalar2=None,
                                                        op0=ALU.add), [tabb, ob], [ob])
    steps = [(ALU.is_ge, float(THRESH[i - 1]), dpos, i) for i in range(2, 16)]
    steps += [(ALU.is_le, -float(THRESH[i - 1]), dneg, i) for i in range(1, 16)]
    for si, (cmp_op, th, dt_, i) in enumerate(steps):
        mk, mkb = mask[si % 2], maskb[si % 2]
        P.op("dve", lambda e, mk=mk, cmp_op=cmp_op, th=th: e.tensor_scalar(out=mk[:, 0:W], in0=rel, scalar1=th, scalar2=None,
                                                                          op0=cmp_op), [relb], [mkb])
        for h in range(4):
            o, ob = outs[h]
            P.op(engs[h], lambda e, o=o, mk=mk, dt_=dt_, i=i, h=h: e.scalar_tensor_tensor(
                out=o[:, 0:W], in0=mk[:, 0:W], scalar=dt_[:, 4 * i + h:4 * i + h + 1], in1=o[:, 0:W],
                op0=ALU.mult, op1=ALU.add), [mkb, tabb, ob], [ob])


def phase_b(C, dr, T):
    P = C.P
    NT = T // 128
    NKT = 4 * NT
    QW = min(512, T)
    NQB = QW // 128
    NQS = T // QW
    KC = min(16, NT)
    with contextlib.ExitStack() as st:
        epsb = make_eps(C, st)
        tb = C.sb(st, [128, 128], F32, "tbl"); dpos = C.sb(st, [128, 64], F32, "dpos"); dneg = C.sb(st, [128, 64], F32, "dneg")
        tabb = P.buf("tab", dma=True)
        P.dma("sp", [(tb[:], dr["tbl"][:].partition_broadcast(128))], [], [tabb], tabb)
        P.op("dve", lambda e: e.tensor_tensor(out=dneg[:, 4:64], in0=tb[:, 4:64], in1=tb[:, 0:60], op=ALU.subtract), [tabb], [tabb])
        P.op("dve", lambda e: e.tensor_tensor(out=dpos[:, 8:64], in0=tb[:, 72:128], in1=tb[:, 68:124], op=ALU.subtract), [tabb], [tabb])
        P.op("dve", lambda e: e.tensor_tensor(out=dpos[:, 4:8], in0=tb[:, 68:72], in1=tb[:, 0:4], op=ALU.subtract), [tabb], [tabb])
        strip = [(C.sb(st, [128, 1152], F32, "strip"), P.buf(f"strip{h}")) for h in range(4)]
        eprev = [(C.sb(st, [128, QW], F32, "eprev"), P.buf(f"eprev{h}")) for h in range(4)]
        enext = [(C.sb(st, [128, QW], F32, "enext"), P.buf(f"enext{h}")) for h in range(4)]
        cbias = [(C.sb(st, [128, 4], F32, "cbias"), P.buf(f"cbias{h}")) for h in range(4)]
        zero = C.sb(st, [128, 1], F32, "zero"); zb = P.buf("zero")
        gsub = C.sb(st, [128, 256], F32, "gsub")
        neglam = C.sb(st, [128, 1], F32, "neglam")
        s2 = contextlib.ExitStack()
        reli = C.sb(s2, [128, 1152], I32, "reli"); rel = C.sb(s2, [128, 1152], F32, "rel"); relb = P.buf("rel", dma=True)
        mask = [C.sb(s2, [128, 1152], F32, "mask") for _ in range(2)]; maskb = [P.buf(f"mask{i}") for i in range(2)]
        P.op("pool", lambda e: e.iota(reli[:], pattern=[[-1, 1152]], base=512, channel_multiplier=1), [], [relb])
        P.op("dve", lambda e: e.tensor_copy(out=rel[:], in_=reli[:]), [relb], [relb])
        bias_eval(C, rel[:], relb, 1152, strip, tb, dpos, dneg, tabb, mask, maskb)
        pqi = C.sb(s2, [128, QW], I32, "pqi"); pki = C.sb(s2, [128, 1], I32, "pki")
        pqf = C.sb(s2, [128, QW], F32, "pqf"); pkf = C.sb(s2, [128, 1], F32, "pkf")
        for (qoff, kj, koff, outs) in ((0, 3, T - 128, eprev), (T - QW, 1, 0, enext)):
            P.dma("sp", [(pqi[:], dr["G_pos"][0, qoff:qoff + QW].partition_broadcast(128)),
                         (pki[:], dr["G_pos"][kj, koff:koff + 128].rearrange("(p o) -> p o", o=1))], [], [relb], relb)
            P.op("dve", lambda e: e.tensor_copy(out=pqf[:], in_=pqi[:]), [relb], [relb])
            P.op("dve", lambda e: e.tensor_copy(out=pkf[:], in_=pki[:]), [relb], [relb])
            P.op("dve", lambda e: e.tensor_scalar(out=rel[:, 0:QW], in0=pqf[:], scalar1=pkf[:, 0:1], scalar2=-1.0,
                                                  op0=ALU.subtract, op1=ALU.mult), [relb], [relb])
            bias_eval(C, rel[:, 0:QW], relb, QW, outs, tb, dpos, dneg, tabb, mask, maskb)
        P.dma("sp", [(pqi[:, 0:4], dr["pmid"][:].partition_broadcast(128))], [], [relb], relb)
        P.op("dve", lambda e: e.tensor_copy(out=pqf[:, 0:4], in_=pqi[:, 0:4]), [relb], [relb])
        P.op("dve", lambda e: e.tensor_scalar(out=rel[:, 0:4], in0=pqf[:, 0:4], scalar1=pqf[:, 0:1], scalar2=None,
                                              op0=ALU.subtract), [relb], [relb])
        bias_eval(C, rel[:, 0:4], relb, 4, cbias, tb, dpos, dneg, tabb, mask, maskb)
        lv = C.sb(s2, [128, 4, 128], F32, "lv"); lvb = P.buf("lv", dma=True)
        lin = C.sb(s2, [128, 2], F32, "lin"); pass
        ee = C.sb(s2, [128, 4], F32, "ee"); pass
        P.dma("sp", [(lv[:, i, :], dr[k][:].partition_broadcast(128)) for i, k in enumerate(("lq1", "lk1", "lq2", "lk2"))]
              + [(lin[:], dr["linit"][:].partition_broadcast(128)), (gsub[:], dr["subln_g"][:].partition_broadcast(128))],
              [], [lvb], lvb)
        P.op("dve", lambda e: e.tensor_tensor(out=lv[:, 0, :], in0=lv[:, 0, :], in1=lv[:, 1, :], op=ALU.mult), [lvb], [lvb])
        P.op("dve", lambda e: e.tensor_tensor(out=lv[:, 2, :], in0=lv[:, 2, :], in1=lv[:, 3, :], op=ALU.mult), [lvb], [lvb])
        P.op("dve", lambda e: e.reduce_sum(out=ee[:, 0:1], in_=lv[:, 0, :], axis=mybir.AxisListType.X), [lvb], [lvb])
        P.op("dve", lambda e: e.reduce_sum(out=ee[:, 1:2], in_=lv[:, 2, :], axis=mybir.AxisListType.X), [lvb], [lvb])
        P.op("act", lambda e: e.activation(out=ee[:, 2:4], in_=ee[:, 0:2], func=AF.Exp), [lvb], [lvb])
        P.op("dve", lambda e: e.tensor_tensor(out=neglam[:], in0=ee[:, 3:4], in1=ee[:, 2:3], op=ALU.subtract), [lvb], [lvb])
        P.op("dve", lambda e: e.tensor_tensor(out=neglam[:], in0=neglam[:], in1=lin[:, 0:1], op=ALU.subtract), [lvb], [lvb])
        P.op("dve", lambda e: e.tensor_scalar(out=gsub[:], in0=gsub[:], scalar1=lin[:, 1:2], scalar2=None, op0=ALU.mult), [lvb], [lvb])
        P.op("pool", lambda e: e.memset(zero[:], 0.0), [], [zb])
        P.barrier()
        s2.close()

        vA = C.sb(st, [128, NKT * 258], BF16, "vA"); vAb = P.buf("vA", dma=True)
        vB = C.sb(st, [128, NKT * 130], BF16, "vB"); vBb = P.buf("vB", dma=True)
        krT = C.sb(st, [64, NKT * 128], BF16, "krT"); krTb = P.buf("krT", dma=True)
        kch = [(C.sb(st, [128, KC * 128], BF16, "kch"), P.buf(f"kch{i}", dma=True)) for i in range(2)]
        qsl = [(C.sb(st, [128, QW], BF16, "qsl"), C.sb(st, [64, QW], BF16, "qrl"), P.buf(f"qsl{i}", dma=True)) for i in range(2)]
        ptl = [(C.sb(st, [128, QW], BF16, "ptl"), P.buf(f"ptl{i}")) for i in range(4)]
        btmp = [(C.sb(st, [128, QW], F32, "btmp"), P.buf(f"btmp{i}")) for i in range(2)]
        o0 = C.sb(st, [128, NQB, 256], F32, "o0"); o0b = P.buf("o0")
        o1 = C.sb(st, [128, 256], F32, "o1"); dd = C.sb(st, [128, 256], F32, "dd"); o1b = P.buf("o1")
        junk = C.sb(st, [128, 256], BF16, "junkB"); junkb = P.buf("junkB")
        rc = C.sb(st, [128, 4], F32, "rc"); rcb = P.buf("rc")
        ss = C.sb(st, [128, 4], F32, "ssB"); ssb = P.buf("ssB")
        rstd = C.sb(st, [128, 4], F32, "rstdB"); rsb = P.buf("rstdB")
        mxs = [(C.sb(st, [128, NQB, 256], BF16, "mxs"), P.buf(f"mxs{i}", dma=True)) for i in range(2)]
        psS = [(C.ps(st, [128, 512], F32, "psS"), P.buf(f"psS{i}")) for i in range(4)]
        psO = [(C.ps(st, [128, 512], F32, "psO"), P.buf(f"psO{i}")) for i in range(4)]
        counters = {"q": 0, "k": 0, "m": 0, "bt": 0}

        def attn_pass(kind, h, m, qs, vt, vtb, dvw, dv):
            mp = 2 * h + m if kind == "d" else h
            scale = 1.0 / math.sqrt(128.0) if kind == "d" else 1.0 / math.sqrt(192.0)
            qt, qr, qb_ = qsl[counters["q"] % 2]; counters["q"] += 1
            q0 = qs * QW
            if kind == "d":
                P.dma("sp", [(qt[:], dr["qT_d"][mp, :, q0:q0 + QW])], [], [qb_], qb_)
            else:
                P.dma("sp", [(qt[:], dr["qT_n"][h, :, q0:q0 + QW]), (qr[:], dr["qT_r"][h, :, q0:q0 + QW])], [], [qb_], qb_)
            v3 = vt[:, 0:NKT * dvw].rearrange("p (t c) -> p t c", c=dvw)
            kcur = [None]
            LA = 2

            def qk(i):
                j, tk = divmod(i, NT)
                if tk % KC == 0:
                    kt_, kb_ = kch[counters["k"] % 2]; counters["k"] += 1
                    src = dr["G_kT_d"][j, mp, :, tk * 128:(tk + KC) * 128] if kind == "d" else \
                        dr["G_kT_n"][j, h, :, tk * 128:(tk + KC) * 128]
                    P.dma("sp", [(kt_[:], src)], [], [kb_], kb_)
                    kcur[0] = (kt_, kb_)
                kt_, kb_ = kcur[0]
                s_t, s_b = psS[i % 4]
                kc0 = (tk % KC) * 128
                if kind == "d":
                    mm(P, s_t[:, 0:QW], kt_[:, kc0:kc0 + 128], qt[:], True, True, [kb_, qb_], [s_b])
                else:
                    mm(P, s_t[:, 0:QW], kt_[:, kc0:kc0 + 128], qt[:], True, False, [kb_, qb_], [s_b])
                    mm(P, s_t[:, 0:QW], krT[:, i * 128:(i + 1) * 128], qr[:], False, True, [krTb, qb_], [s_b])

            def ex(i):
                j, tk = divmod(i, NT)
                s_t, s_b = psS[i % 4]
                p_t, p_b = ptl[i % 4]
                band = None
                bcol = None
                bbuf = zb
                if kind == "d":
                    if j == 0:
                        dl = tk - NQB * qs
                        if dl < -1:
                            bcol = tb[:, 15 * 4 + h:15 * 4 + h + 1]; bbuf = tabb
                        elif dl > NQB:
                            bcol = tb[:, 31 * 4 + h:31 * 4 + h + 1]; bbuf = tabb
                        else:
                            band = (strip[h][0][:, 512 - 128 * dl:512 - 128 * dl + QW], strip[h][1])
                    elif j == 3 and tk == NT - 1 and qs == 0:
                        band = (eprev[h][0][:, :], eprev[h][1])
                    elif j == 1 and tk == 0 and qs == NQS - 1:
                        band = (enext[h][0][:, :], enext[h][1])
                    else:
                        bcol = cbias[h][0][:, j:j + 1]; bbuf = cbias[h][1]
                else:
                    bcol = zero[:, 0:1]
                if band is not None:
                    bt_, btb_ = btmp[counters["bt"] % 2]; counters["bt"] += 1
                    P.op("dve", lambda e: e.scalar_tensor_tensor(out=bt_[:], in0=s_t[:, 0:QW], scalar=scale, in1=band[0],
                                                                op0=ALU.mult, op1=ALU.add), [s_b, band[1]], [btb_])
                    P.op("act", lambda e: e.activation(out=p_t[:], in_=bt_[:], func=AF.Exp, bias=zero[:, 0:1]), [btb_, zb], [p_b])
                else:
                    P.op("act", lambda e: e.activation(out=p_t[:], in_=s_t[:, 0:QW], func=AF.Exp, bias=bcol, scale=scale),
                         [s_b, bbuf], [p_b])

            def pv(i):
                p_t, p_b = ptl[i % 4]
                for qb in range(NQB):
                    o_t, o_b = psO[qb]
                    mm(P, o_t[:, 0:dv + 1], p_t[:, qb * 128:(qb + 1) * 128], v3[:, i, 0:dv + 1], i == 0, i == NKT - 1,
                       [p_b, vtb], [o_b])

            for i in range(NKT + LA):
                if i < NKT:
                    qk(i)
                    ex(i)
                if i >= LA:
                    pv(i - LA)
            for qb in range(NQB):
                o_t, o_b = psO[qb]
                P.op("dve", lambda e, o_t=o_t, qb=qb: e.reciprocal(out=rc[:, qb:qb + 1], in_=o_t[:, dv:dv + 1]), [o_b], [rcb])
                if kind == "m":
                    mt, mb = mxs[counters["m"] % 2]
                    P.op("dve", lambda e, o_t=o_t, qb=qb, mt=mt: e.tensor_scalar(out=mt[:, qb, 0:128], in0=o_t[:, 0:128],
                                                                               scalar1=rc[:, qb:qb + 1], scalar2=None, op0=ALU.mult),
                         [o_b, rcb], [mb])
                elif m == 0:
                    P.op("dve", lambda e, o_t=o_t, qb=qb: e.tensor_scalar(out=o0[:, qb, :], in0=o_t[:, 0:256], scalar1=rc[:, qb:qb + 1],
                                                                        scalar2=None, op0=ALU.mult), [o_b, rcb], [o0b])
                else:
                    mt, mb = mxs[counters["m"] % 2]
                    P.op("dve", lambda e, o_t=o_t, qb=qb: e.tensor_scalar(out=o1[:], in0=o_t[:, 0:256], scalar1=rc[:, qb:qb + 1],
                                                                        scalar2=None, op0=ALU.mult), [o_b, rcb], [o1b])
                    P.op("dve", lambda e, qb=qb: e.scalar_tensor_tensor(out=dd[:], in0=o1[:], scalar=neglam[:, 0:1], in1=o0[:, qb, :],
                                                                       op0=ALU.mult, op1=ALU.add), [o1b, o0b, lvb], [o1b])
                    P.op("act", lambda e, qb=qb: e.activation(out=junk[:], in_=dd[:], func=AF.Square, accum_out=ss[:, qb:qb + 1]),
                         [o1b], [junkb, ssb])
                    rms_rstd(C, ss[:, qb:qb + 1], rstd[:, qb:qb + 1], 256, ssb, rsb, epsb)
                    P.op("dve", lambda e, qb=qb, mt=mt: e.scalar_tensor_tensor(out=mt[:, qb, :], in0=dd[:], scalar=rstd[:, qb:qb + 1],
                                                                              in1=gsub[:], op0=ALU.mult, op1=ALU.mult),
                         [o1b, rsb, lvb], [mb])
            if kind == "m":
                mt, mb = mxs[counters["m"] % 2]; counters["m"] += 1
                P.dma("sp", [(dr["mix"][q0:q0 + QW, 1024 + h * 128:1024 + (h + 1) * 128].rearrange("(b p) c -> p b c", p=128),
                              mt[:, :, 0:128])], [mb], [], mb)
            elif m == 1:
                mt, mb = mxs[counters["m"] % 2]; counters["m"] += 1
                P.dma("sp", [(dr["mix"][q0:q0 + QW, h * 256:(h + 1) * 256].rearrange("(b p) c -> p b c", p=128), mt[:])],
                      [mb], [], mb)

        for h in range(NH_D):
            P.dma("sp", [(vA[:, j * NT * 258:(j + 1) * NT * 258], dr["G_v_d"][j, h].rearrange("p t c -> p (t c)")) for j in range(4)],
                  [], [vAb], vAb)
            for qs in range(NQS):
                for m in range(2):
                    attn_pass("d", h, m, qs, vA, vAb, 258, 256)
        P.dma("sp", [(krT[:, j * T:(j + 1) * T], dr["G_kT_r"][j]) for j in range(4)], [], [krTb], krTb)
        for h in range(NH_M):
            vt, vtb = (vA, vAb) if h % 2 == 0 else (vB, vBb)
            P.dma("sp", [(vt[:, j * NT * 130:(j + 1) * NT * 130], dr["G_v_m"][j, h].rearrange("p t c -> p (t c)")) for j in range(4)],
                  [], [vtb], vtb)
            for qs in range(NQS):
                attn_pass("m", h, 0, qs, vt, vtb, 130, 128)
        P.barrier()
def phase_c(C, dr, T, last):
    P = C.P
    TT = min(512, T)
    NB = TT // 128
    with contextlib.ExitStack() as st:
        modt = C.sb(st, [128, 8192], F32, "modC")
        modb = P.buf("modC", dma=True)
        load_mod(C, dr, 4096, 8192, modt, modb, [(4096, 6144)])
        ga, shm, scm, gm = modt[:, 0:2048], modt[:, 2048:4096], modt[:, 4096:6144], modt[:, 6144:8192]
        ident, identb = make_ident(C, st)
        epsb = make_eps(C, st)
        xt = C.sb(st, [128, NB, 2048], F32, "xtC"); xb = P.buf("xtC", dma=True)
        hb_t = C.sb(st, [128, NB, 2048], BF16, "hbC"); hbb = P.buf("hbC", dma=True)
        hT = C.sb(st, [128, 16, TT], BF16, "hTC"); hTb = P.buf("hTC")
        hid = C.sb(st, [128, 32, TT], BF16, "hid"); hidb = P.buf("hid")
        wg = [(C.sb(st, [128, 16, 512], BF16, "wgC"), P.buf(f"wgC{i}", dma=True)) for i in range(3)]
        tmpf = C.sb(st, [128, 2048], F32, "tmpfC"); tmpfb = P.buf("tmpfC")
        junk = C.sb(st, [128, 2048], BF16, "junkC"); junkb = P.buf("junkC")
        ev = [(C.sb(st, [128, 512], F32, "evC"), P.buf(f"evC{i}")) for i in range(2)]
        rl = [(C.sb(st, [128, 512], F32, "rlC"), P.buf(f"rlC{i}")) for i in range(2)]
        ss = C.sb(st, [128, 8], F32, "ssC"); ssb = P.buf("ssC")
        rstd = C.sb(st, [128, 8], F32, "rstdC"); rsb = P.buf("rstdC")
        ps = [(C.ps(st, [128, 512], F32, "psC"), P.buf(f"psC{i}")) for i in range(6)]
        ptb = [(C.ps(st, [128, 1024], BF16, "ptC"), P.buf(f"ptC{i}")) for i in range(2)]
        if last:
            fg = C.sb(st, [128, 2048], F32, "fg"); fgb = P.buf("fg", dma=True)
            P.dma("sp", [(fg[:], dr["final_g"][:].partition_broadcast(128))], [], [fgb], fgb)
        wgi = [0]
        evi = [0]
        pci = [0]

        def load_w(key, r0, c0):
            wt, wb = wg[wgi[0] % 3]
            wgi[0] += 1
            src = dr[key][r0:r0 + 2048, c0:c0 + 512].rearrange("(kc p) n -> p kc n", p=128)
            P.dma("sp", [(wt[:], src)], [], [wb], wb)
            return wt, wb

        def resid_evac(pt_, pb_, g_ap, xs):
            et, eb = ev[evi[0] % 2]
            evi[0] += 1
            P.op("dve", lambda e: e.tensor_tensor(out=et[:], in0=pt_[:], in1=g_ap, op=ALU.mult), [pb_, modb], [eb])
            P.op("pool", lambda e: e.tensor_tensor(out=xs, in0=xs, in1=et[:], op=ALU.add), [eb, xb], [xb])

        for tt in range(T // TT):
            t0 = tt * TT
            P.dma("sp", [(xt[:], dr["x"][t0:t0 + TT, :].rearrange("(b p) d -> p b d", p=128))], [], [xb], xb)
            P.dma("sp", [(hb_t[:], dr["mix"][t0:t0 + TT, :].rearrange("(b p) d -> p b d", p=128))], [], [hbb], hbb)
            for tb in range(NB):
                transpose_blocks(C, lambda k, tb=tb: hb_t[:, tb, k * 128:(k + 1) * 128], 16, hT, hTb, hbb, tb, ptb,
                                 ident, identb, ["act", "dve"])
            for cg in range(4):
                wt, wb = load_w("wb_o", 0, cg * 512)
                for tb in range(NB):
                    pci[0] += 1
                    pt_, pb_ = ps[pci[0] % 6]
                    for kc in range(16):
                        mm(P, pt_[:], hT[:, kc, tb * 128:(tb + 1) * 128], wt[:, kc, :], kc == 0, kc == 15, [wb, hTb], [pb_])
                    resid_evac(pt_, pb_, ga[:, cg * 512:(cg + 1) * 512], xt[:, tb, cg * 512:(cg + 1) * 512])
            norm_mod_transpose(C, xt, xb, NB, shm, scm, modb, hb_t, hbb, hT, hTb, tmpf, tmpfb, junk, junkb,
                               ss, ssb, rstd, rsb, epsb, ptb, ident, identb)
            for half in range(2):
                for g in range(8):
                    wt, wb = load_w("wb_m1", 0, half * 4096 + g * 512)
                    for oc in range(4):
                        pci[0] += 1
                        pt_, pb_ = ps[pci[0] % 2]
                        for kc in range(16):
                            mm(P, pt_[:, 0:TT], wt[:, kc, oc * 128:(oc + 1) * 128], hT[:, kc, :], kc == 0, kc == 15, [wb, hTb], [pb_])
                        rt, rb_ = rl[(g * 4 + oc) % 2]
                        P.op("act", lambda e, rt=rt, pt_=pt_: e.activation(out=rt[:, 0:TT], in_=pt_[:, 0:TT], func=AF.Relu), [pb_], [rb_])
                        P.op("pool", lambda e, rt=rt, g=g, oc=oc: e.tensor_tensor(out=hid[:, g * 4 + oc, :], in0=rt[:, 0:TT], in1=rt[:, 0:TT],
                                                                                op=ALU.mult), [rb_], [hidb])
                for cg in range(4):
                    accs = [ps[2 + tb] for tb in range(NB)]
                    for kg in range(2):
                        wt, wb = load_w("wb_m2", half * 4096 + kg * 2048, cg * 512)
                        for tb in range(NB):
                            pt_, pb_ = accs[tb]
                            for kc in range(16):
                                mm(P, pt_[:], hid[:, kg * 16 + kc, tb * 128:(tb + 1) * 128], wt[:, kc, :],
                                   kg == 0 and kc == 0, kg == 1 and kc == 15, [wb, hidb], [pb_])
                    for tb in range(NB):
                        pt_, pb_ = accs[tb]
                        resid_evac(pt_, pb_, gm[:, cg * 512:(cg + 1) * 512], xt[:, tb, cg * 512:(cg + 1) * 512])
            if last:
                for tb in range(NB):
                    P.op("act", lambda e, tb=tb: e.activation(out=junk[:], in_=xt[:, tb, :], func=AF.Square,
                                                             accum_out=ss[:, tb:tb + 1]), [xb], [junkb, ssb])
                rms_rstd(C, ss[:, 0:NB], rstd[:, 0:NB], D, ssb, rsb, epsb)
                for tb in range(NB):
                    P.op("dve", lambda e, tb=tb: e.scalar_tensor_tensor(out=xt[:, tb, :], in0=xt[:, tb, :], scalar=rstd[:, tb:tb + 1],
                                                                       in1=fg[:], op0=ALU.mult, op1=ALU.mult), [xb, rsb, fgb], [xb])
            P.dma("sp", [(dr["x_out"][t0:t0 + TT, :].rearrange("(b p) d -> p b d", p=128), xt[:])], [xb], [], xb)
        P.barrier()
def _new_nc():
    return bass.Bass("TRN2", target_bir_lowering=False)


def _din(nc, name, shape, dt):
    return nc.dram_tensor(name, list(shape), dt, kind="ExternalInput").ap()


def _dout(nc, name, shape, dt):
    return nc.dram_tensor(name, list(shape), dt, kind="ExternalOutput").ap()


def _finish(nc, es, C):
    blk = es.enter_context(nc.Block())
    C.P.emit(blk)


def build_0(NW, NS, depth):
    nc = _new_nc()
    with contextlib.ExitStack() as es:
        C = Ctx(nc, es)
        dr = {"sgn": _din(nc, "sgn", [128, 512], F32), "wflat": _din(nc, "wflat", [128, NW], F32),
              "wsig": _din(nc, "wsig", [128, NS], F32), "cT2": _din(nc, "cT2", [128, 16, 2], F32),
              "wada": _din(nc, "wada", [depth, D, 1536], F32), "bsl": _din(nc, "bsl", [2, depth, 1536], F32),
              "wbflat": _dout(nc, "wbflat", [128, NW], BF16), "wbsig": _dout(nc, "wbsig", [128, NS], BF16),
              "mod_out": _dout(nc, "mod_out", [2, depth, 1536], F32)}
        phase0(C, dr, NW, NS, depth)
        _finish(nc, es, C)
    return nc


def build_a(T):
    nc = _new_nc()
    NT = T // 128
    with contextlib.ExitStack() as es:
        C = Ctx(nc, es)
        dr = {"x": _din(nc, "x", [T, D], F32), "modv": _din(nc, "modv", [6 * D], F32), "pos": _din(nc, "pos", [1, T], I32),
              "wb_in": _din(nc, "wb_in", [D, WIN_COLS], BF16), "wb_uq": _din(nc, "wb_uq", [512, WUQ_COLS], BF16),
              "wb_ukv": _din(nc, "wb_ukv", [256, 2048], BF16), "q_norm_g": _din(nc, "q_norm_g", [512], F32),
              "kv_norm_g": _din(nc, "kv_norm_g", [256], F32), "invf": _din(nc, "invf", [64, 1], F32),
              "qT_d": _dout(nc, "qT_d", [8, 128, T], BF16), "qT_n": _dout(nc, "qT_n", [8, 128, T], BF16),
              "qT_r": _dout(nc, "qT_r", [8, 64, T], BF16), "kT_d": _dout(nc, "kT_d", [8, 128, T], BF16),
              "kT_n": _dout(nc, "kT_n", [8, 128, T], BF16), "kT_r": _dout(nc, "kT_r", [64, T], BF16),
              "v_d": _dout(nc, "v_d", [4, 128, NT, 258], BF16), "v_m": _dout(nc, "v_m", [8, 128, NT, 130], BF16)}
        phase_a(C, dr, T)
        _finish(nc, es, C)
    return nc


def build_b(T):
    nc = _new_nc()
    NT = T // 128
    with contextlib.ExitStack() as es:
        C = Ctx(nc, es)
        dr = {"qT_d": _din(nc, "qT_d", [8, 128, T], BF16), "qT_n": _din(nc, "qT_n", [8, 128, T], BF16),
              "qT_r": _din(nc, "qT_r", [8, 64, T], BF16),
              "G_kT_d": _din(nc, "G_kT_d", [4, 8, 128, T], BF16), "G_kT_n": _din(nc, "G_kT_n", [4, 8, 128, T], BF16),
              "G_kT_r": _din(nc, "G_kT_r", [4, 64, T], BF16), "G_v_d": _din(nc, "G_v_d", [4, 4, 128, NT, 258], BF16),
              "G_v_m": _din(nc, "G_v_m", [4, 8, 128, NT, 130], BF16), "G_pos": _din(nc, "G_pos", [4, T], I32),
              "pmid": _din(nc, "pmid", [4], I32), "tbl": _din(nc, "tbl", [128], F32),
              "lq1": _din(nc, "lq1", [128], F32), "lk1": _din(nc, "lk1", [128], F32), "lq2": _din(nc, "lq2", [128], F32),
              "lk2": _din(nc, "lk2", [128], F32), "subln_g": _din(nc, "subln_g", [256], F32), "linit": _din(nc, "linit", [2], F32),
              "mix": _dout(nc, "mix", [T, D], BF16)}
        phase_b(C, dr, T)
        _finish(nc, es, C)
    return nc


def build_c(T, last):
    nc = _new_nc()
    with contextlib.ExitStack() as es:
        C = Ctx(nc, es)
        dr = {"x": _din(nc, "x", [T, D], F32), "mix": _din(nc, "mix", [T, D], BF16), "modv": _din(nc, "modv", [6 * D], F32),
              "wb_o": _din(nc, "wb_o", [D, D], BF16), "wb_m1": _din(nc, "wb_m1", [D, DFF], BF16),
              "wb_m2": _din(nc, "wb_m2", [DFF, D], BF16), "x_out": _dout(nc, "x_out", [T, D], F32)}
        if last:
            dr["final_g"] = _din(nc, "final_g", [D], F32)
        phase_c(C, dr, T, last)
        _finish(nc, es, C)
    return nc


def _run(nc, in_maps):
    import time, os
    t = time.time()
    res = run_bass_kernel_spmd(nc, in_maps, core_ids=list(range(8)))
    if os.environ.get("KVERB"):
        print("[kernel] launch took %.1fs" % (time.time() - t), flush=True)
    return res.results


def _lambda_init(layer_idx):
    return 0.8 - 0.6 * math.exp(-0.3 * (layer_idx - 1))


def kernel(x, c, positions, rel_bias_table, w_ada, b_ada, w_in, diff_lambda_q1, diff_lambda_k1, diff_lambda_q2,
           diff_lambda_k2, diff_subln_g, q_norm_g, w_uq, kv_norm_g, w_ukv, w_o, w_mlp_in, w_mlp_out, final_norm_g):
    f32 = np.float32
    x = np.asarray(x, f32)
    B, S, _ = x.shape
    T = S // 4
    depth = w_in.shape[0]
    positions = np.asarray(positions, np.int32)
    c = np.asarray(c, f32)
    sgn = np.tile(np.concatenate([-np.ones(32, f32), np.ones(32, f32)]), 8)[None, :].repeat(128, 0)
    invf = (1.0 / (np.float32(10000.0) ** (np.arange(0, 64, 2, dtype=f32) / np.float32(64)))).astype(f32)
    invf = np.concatenate([invf, invf]).reshape(64, 1).astype(f32)
    xs = [np.ascontiguousarray(x[k // 4, (k % 4) * T:(k % 4 + 1) * T]) for k in range(8)]
    poss = [np.ascontiguousarray(positions[k // 4, (k % 4) * T:(k % 4 + 1) * T]).reshape(1, T) for k in range(8)]
    uns, sig = [], []
    for l in range(depth):
        wi = np.asarray(w_in[l], f32)
        wq = np.asarray(w_uq[l], f32)
        kr = wi[:, OFF_KR:OFF_KR + 64]
        qr = wq.reshape(512, 8, 192)[:, :, 128:]
        uns += [wi, wq, np.asarray(w_ukv[l], f32), np.asarray(w_o[l], f32), np.asarray(w_mlp_in[l], f32),
                np.asarray(w_mlp_out[l], f32)]
        sig += [np.concatenate([kr[:, 32:], kr[:, :32]], axis=1),
                np.concatenate([qr[:, :, 32:], qr[:, :, :32]], axis=2).reshape(512, 512)]

    def shard(ws, k):
        return np.ascontiguousarray(np.concatenate([w.reshape(8, 128, -1)[k] for w in ws], axis=1))

    def unshard(parts, ws):
        outs, c0 = [], 0
        for w in ws:
            n = w.size // 1024
            outs.append(np.stack([p[:, c0:c0 + n] for p in parts]).reshape(w.shape))
            c0 += n
        return outs

    NW = sum(w.size for w in uns) // 1024
    NS = sum(w.size for w in sig) // 1024
    cT2 = np.ascontiguousarray(c.reshape(B, 16, 128).transpose(2, 1, 0))
    wada_f = np.asarray(w_ada, f32)
    bada_f = np.asarray(b_ada, f32)
    maps0 = []
    for k in range(8):
        sl = slice(1536 * k, 1536 * (k + 1))
        maps0.append({"sgn": sgn, "wflat": shard(uns, k), "wsig": shard(sig, k), "cT2": cT2,
                      "wada": np.ascontiguousarray(wada_f[:, :, sl]),
                      "bsl": np.ascontiguousarray(np.broadcast_to(bada_f[None, :, sl], (2, depth, 1536)))})
    r0 = _run(build_0(NW, NS, depth), maps0)
    wbu = unshard([np.asarray(r0[k]["wbflat"]) for k in range(8)], uns)
    wbs = unshard([np.asarray(r0[k]["wbsig"]) for k in range(8)], sig)
    mods = np.concatenate([np.asarray(r0[k]["mod_out"], f32) for k in range(8)], axis=2)
    nca, ncb = build_a(T), build_b(T)
    for l in range(depth):
        wb = {"wb_in": np.ascontiguousarray(np.concatenate([wbu[6 * l], wbs[2 * l]], axis=1)),
              "wb_uq": np.ascontiguousarray(np.concatenate([wbu[6 * l + 1], wbs[2 * l + 1]], axis=1)),
              "wb_ukv": wbu[6 * l + 2], "wb_o": wbu[6 * l + 3], "wb_m1": wbu[6 * l + 4], "wb_m2": wbu[6 * l + 5]}
        amaps = [{"x": xs[k], "modv": np.ascontiguousarray(mods[k // 4, l]), "pos": poss[k], "wb_in": wb["wb_in"],
                  "wb_uq": wb["wb_uq"], "wb_ukv": wb["wb_ukv"], "q_norm_g": np.asarray(q_norm_g[l], f32),
                  "kv_norm_g": np.asarray(kv_norm_g[l], f32), "invf": invf} for k in range(8)]
        ra = _run(nca, amaps)
        li = _lambda_init(l + 1)
        linit = np.array([li, 1.0 - li], f32)
        bmaps = []
        for k in range(8):
            b, r = divmod(k, 4)
            order = [b * 4 + (r + j) % 4 for j in range(4)]
            gp = np.stack([poss[o][0] for o in order])
            bmaps.append({"qT_d": ra[k]["qT_d"], "qT_n": ra[k]["qT_n"], "qT_r": ra[k]["qT_r"],
                          "G_kT_d": np.stack([ra[o]["kT_d"] for o in order]), "G_kT_n": np.stack([ra[o]["kT_n"] for o in order]),
                          "G_kT_r": np.stack([ra[o]["kT_r"] for o in order]), "G_v_d": np.stack([ra[o]["v_d"] for o in order]),
                          "G_v_m": np.stack([ra[o]["v_m"] for o in order]), "G_pos": gp,
                          "pmid": np.ascontiguousarray(gp[:, T // 2]), "tbl": np.asarray(rel_bias_table, f32).reshape(128),
                          "lq1": np.asarray(diff_lambda_q1[l], f32), "lk1": np.asarray(diff_lambda_k1[l], f32),
                          "lq2": np.asarray(diff_lambda_q2[l], f32), "lk2": np.asarray(diff_lambda_k2[l], f32),
                          "subln_g": np.asarray(diff_subln_g[l], f32), "linit": linit})
        rb = _run(ncb, bmaps)
        last = l == depth - 1
        ncc = build_c(T, last)
        cmaps = []
        for k in range(8):
            m = {"x": xs[k], "mix": rb[k]["mix"], "modv": np.ascontiguousarray(mods[k // 4, l]), "wb_o": wb["wb_o"],
                 "wb_m1": wb["wb_m1"], "wb_m2": wb["wb_m2"]}
            if last:
                m["final_g"] = np.asarray(final_norm_g, f32)
            cmaps.append(m)
        rc = _run(ncc, cmaps)
        xs = [np.asarray(rc[k]["x_out"], f32) for k in range(8)]
    out = np.empty((B, S, D), f32)
    for k in range(8):
        out[k // 4, (k % 4) * T:(k % 4 + 1) * T] = xs[k]
    return out
```
